# Optimizing a Trainium2 kernel written in Bass

```python
import jax, jax.numpy as jnp
from jax import lax
import numpy as np

D_MODEL = 2048
BATCH = 2
SEQ = 16384
DEPTH = 2

CHUNK = 64
Q_BLOCK = 128
HEAD_DIM = 128
D_MIX = D_MODEL
RET_HEADS = D_MIX // 4 // HEAD_DIM
FOX_HEADS = D_MIX // 2 // HEAD_DIM
MLSTM_HEADS = D_MIX // 4 // HEAD_DIM
RET_W = RET_HEADS * HEAD_DIM
FOX_W = FOX_HEADS * HEAD_DIM
MLSTM_W = MLSTM_HEADS * HEAD_DIM
CONV_WIDTH = 4
D_FF = 4 * D_MODEL
ROPE_BASE = 10000.0
NORM_EPS = 1e-6
N_MOD = 6
IN_SIZES = [RET_W] * 4 + [FOX_W] * 3 + [FOX_HEADS] + [MLSTM_W] * 4 + [MLSTM_HEADS, MLSTM_HEADS]
D_IN = sum(IN_SIZES)
IN_SPLIT_IDX = [int(i) for i in np.cumsum(IN_SIZES)[:-1]]

kernel_name = "hymba_retention_fox_mlstm_trunk"


def rmsnorm(x, w):
    x32 = x.astype(jnp.float32)
    y = x32 * lax.rsqrt(jnp.mean(x32 * x32, axis=-1, keepdims=True) + NORM_EPS)
    return (y * w.astype(jnp.float32)).astype(x.dtype)


def head_norm(y):
    B, S, H, d = y.shape
    y32 = y.astype(jnp.float32)
    mu = jnp.mean(y32, axis=-1, keepdims=True)
    var = jnp.mean(jnp.square(y32 - mu), axis=-1, keepdims=True)
    return ((y32 - mu) * lax.rsqrt(var + NORM_EPS)).reshape(B, S, H * d).astype(y.dtype)


def rotary(t, positions):
    d = t.shape[-1]
    inv_freq = ROPE_BASE ** (-jnp.arange(0, d, 2, dtype=jnp.float32) / d)
    ang = positions.astype(jnp.float32)[:, :, None] * inv_freq
    cos = jnp.cos(ang)[:, :, None, :]
    sin = jnp.sin(ang)[:, :, None, :]
    t32 = t.astype(jnp.float32)
    t1, t2 = t32[..., : d // 2], t32[..., d // 2:]
    return jnp.concatenate([t1 * cos - t2 * sin, t2 * cos + t1 * sin], axis=-1).astype(t.dtype)


def to_chunks(t):
    B, S, H, d = t.shape
    return t.reshape(B, S // CHUNK, CHUNK, H, d).transpose(1, 0, 3, 2, 4)


def from_chunks(t):
    NC, B, H, L, d = t.shape
    return t.transpose(1, 0, 3, 2, 4).reshape(B, NC * L, H, d)


def gate_chunks(g):
    B, S, H = g.shape
    return g.reshape(B, S // CHUNK, CHUNK, H).transpose(1, 0, 3, 2)


def causal_conv(x, w, b):
    K, C = w.shape
    y = lax.conv_general_dilated(x, w[:, None, :].astype(x.dtype), window_strides=(1,),
                                 padding=[(K - 1, 0)], dimension_numbers=('NWC', 'WIO', 'NWC'),
                                 feature_group_count=C)
    return y + b.astype(x.dtype)


def retention(q, k, v, positions):
    B, S, H, d = q.shape
    q = rotary(q, positions)
    k = rotary(k, positions) * (d ** -0.5)
    log_gamma = jnp.log(1.0 - 2.0 ** (-5.0 - jnp.arange(H, dtype=jnp.float32)))
    idx = jnp.arange(CHUNK, dtype=jnp.float32)
    rel = idx[:, None] - idx[None, :]
    intra_decay = jnp.where(rel >= 0, jnp.exp(jnp.maximum(rel, 0.0)[None] * log_gamma[:, None, None]), 0.0)
    q_decay = jnp.exp((idx + 1.0)[None, :] * log_gamma[:, None])[..., None]
    k_decay = jnp.exp((CHUNK - 1.0 - idx)[None, :] * log_gamma[:, None])[..., None]
    chunk_decay = jnp.exp(CHUNK * log_gamma)[:, None, None]

    def step(state, inp):
        qi, ki, vi = inp
        scores = jnp.einsum('bhld,bhmd->bhlm', qi, ki) * intra_decay
        intra = jnp.einsum('bhlm,bhmd->bhld', scores, vi)
        inter = jnp.einsum('bhld,bhde->bhle', qi * q_decay, state)
        new_state = state * chunk_decay + jnp.einsum('bhmd,bhme->bhde', ki * k_decay, vi)
        return new_state, intra + inter

    state0 = jnp.zeros((B, H, d, d), jnp.float32)
    _, out = lax.scan(step, state0, (to_chunks(q), to_chunks(k), to_chunks(v)))
    return from_chunks(out).astype(q.dtype)


def forgetting_attention(q, k, v, f_logit):
    B, S, H, d = q.shape
    cum = jnp.cumsum(jax.nn.log_sigmoid(f_logit.astype(jnp.float32)), axis=1).transpose(0, 2, 1)
    q_h = q.transpose(0, 2, 1, 3) * (d ** -0.5)
    k_h = k.transpose(0, 2, 1, 3)
    v_h = v.transpose(0, 2, 1, 3)
    nb = S // Q_BLOCK
    qb = q_h.reshape(B, H, nb, Q_BLOCK, d).transpose(2, 0, 1, 3, 4)
    cb = cum.reshape(B, H, nb, Q_BLOCK).transpose(2, 0, 1, 3)
    qpos = jnp.arange(S).reshape(nb, Q_BLOCK)
    kpos = jnp.arange(S)

    def block(inp):
        qi, ci, pi = inp
        logits = jnp.einsum('bhqd,bhkd->bhqk', qi, k_h).astype(jnp.float32) + ci[..., None] - cum[:, :, None, :]
        logits = jnp.where((pi[:, None] >= kpos[None, :])[None, None], logits, -jnp.inf)
        p = jax.nn.softmax(logits, axis=-1)
        return jnp.einsum('bhqk,bhkd->bhqd', p.astype(v_h.dtype), v_h)

    out = lax.map(block, (qb, cb, qpos))
    return out.transpose(1, 0, 3, 2, 4).reshape(B, S, H, d)


def mlstm(q, k, v, i_logit, f_logit):
    B, S, H, d = q.shape
    k = k * (d ** -0.5)
    ic = gate_chunks(i_logit.astype(jnp.float32))
    lfc = gate_chunks(jax.nn.log_sigmoid(f_logit.astype(jnp.float32)))
    causal = jnp.tril(jnp.ones((CHUNK, CHUNK), dtype=bool))

    def step(carry, inp):
        C, n, m = carry
        qi, ki, vi, ii, lfi = inp
        b = jnp.cumsum(lfi, axis=-1)
        D = jnp.where(causal, b[..., :, None] - b[..., None, :] + ii[..., None, :], -jnp.inf)
        inter_log = b + m[..., None]
        m_q = jnp.maximum(inter_log, jnp.max(D, axis=-1))
        w_intra = jnp.exp(D - m_q[..., None])
        w_inter = jnp.exp(inter_log - m_q)
        s = jnp.einsum('bhlk,bhmk->bhlm', qi, ki) * w_intra
        num = jnp.einsum('bhlm,bhmv->bhlv', s, vi) + w_inter[..., None] * jnp.einsum('bhlk,bhkv->bhlv', qi, C)
        den = jnp.sum(s, axis=-1) + w_inter * jnp.einsum('bhlk,bhk->bhl', qi, n)
        h = num / jnp.maximum(jnp.abs(den), jnp.exp(-m_q))[..., None]
        b_last = b[..., -1]
        k_log = b_last[..., None] - b + ii
        m_new = jnp.maximum(b_last + m, jnp.max(k_log, axis=-1))
        wk = jnp.exp(k_log - m_new[..., None])
        carry_scale = jnp.exp(b_last + m - m_new)
        C_new = carry_scale[..., None, None] * C + jnp.einsum('bhl,bhlk,bhlv->bhkv', wk, ki, vi)
        n_new = carry_scale[..., None] * n + jnp.einsum('bhl,bhlk->bhk', wk, ki)
        return (C_new, n_new, m_new), h

    carry0 = (jnp.zeros((B, H, d, d), jnp.float32), jnp.zeros((B, H, d), jnp.float32),
              jnp.zeros((B, H), jnp.float32))
    _, out = lax.scan(step, carry0, (to_chunks(q), to_chunks(k), to_chunks(v), ic, lfc))
    return from_chunks(out).astype(q.dtype)


def hybrid_mixer(h, positions, w_in, conv_w, conv_b, fox_f_bias, mlstm_i_bias, mlstm_f_bias,
                 merge_scale, w_out):
    B, S, _ = h.shape
    heads = lambda t, H: t.reshape(B, S, H, HEAD_DIM)
    proj = h @ w_in
    (rq, rk, rv, rg, fq, fk, fv, ff, mq, mk, mv, mo, mi, mf) = jnp.split(proj, IN_SPLIT_IDX, axis=-1)
    y_ret = head_norm(retention(heads(rq, RET_HEADS), heads(rk, RET_HEADS), heads(rv, RET_HEADS), positions))
    y_ret = y_ret * jax.nn.silu(rg)
    y_fox = head_norm(forgetting_attention(heads(fq, FOX_HEADS), heads(fk, FOX_HEADS), heads(fv, FOX_HEADS),
                                           ff + fox_f_bias))
    mqk = jax.nn.silu(causal_conv(jnp.concatenate([mq, mk], axis=-1), conv_w, conv_b))
    mq, mk = jnp.split(mqk, 2, axis=-1)
    y_m = head_norm(mlstm(heads(mq, MLSTM_HEADS), heads(mk, MLSTM_HEADS), heads(mv, MLSTM_HEADS),
                          mi + mlstm_i_bias, mf + mlstm_f_bias))
    y_m = y_m * jax.nn.sigmoid(mo)
    y = jnp.concatenate([y_ret, y_fox, y_m], axis=-1) * merge_scale
    return y @ w_out


def setup_inputs(seed: int = 0) -> dict:
    key = jax.random.key(seed)
    ks = jax.random.split(key, 20)
    nrm = lambda k, shape, scale: jax.random.normal(k, shape, jnp.float32) * scale
    x = nrm(ks[0], (BATCH, SEQ, D_MODEL), 1.0)
    c = nrm(ks[1], (BATCH, D_MODEL), 1.0)
    offsets = jax.random.randint(ks[2], (BATCH, 1), 0, 1024) * CHUNK
    positions = (offsets + jnp.arange(SEQ)[None, :]).astype(jnp.int32)
    return {
        "x": x,
        "c": c,
        "positions": positions,
        "ada_w": nrm(ks[3], (DEPTH, D_MODEL, N_MOD * D_MODEL), 0.5 * D_MODEL ** -0.5),
        "ada_b": nrm(ks[4], (DEPTH, N_MOD * D_MODEL), 0.02),
        "norm_mix_w": 1.0 + nrm(ks[5], (DEPTH, D_MODEL), 0.02),
        "norm_mlp_w": 1.0 + nrm(ks[6], (DEPTH, D_MODEL), 0.02),
        "w_in": nrm(ks[7], (DEPTH, D_MODEL, D_IN), D_MODEL ** -0.5),
        "conv_w": nrm(ks[8], (DEPTH, CONV_WIDTH, 2 * MLSTM_W), CONV_WIDTH ** -0.5),
        "conv_b": nrm(ks[9], (DEPTH, 2 * MLSTM_W), 0.02),
        "fox_f_bias": jax.random.uniform(ks[10], (DEPTH, FOX_HEADS), jnp.float32, 1.0, 4.0),
        "mlstm_i_bias": nrm(ks[11], (DEPTH, MLSTM_HEADS), 0.1),
        "mlstm_f_bias": jax.random.uniform(ks[12], (DEPTH, MLSTM_HEADS), jnp.float32, 3.0, 6.0),
        "merge_scale": 1.0 + nrm(ks[13], (DEPTH, D_MIX), 0.02),
        "w_out": nrm(ks[14], (DEPTH, D_MIX, D_MODEL), D_MIX ** -0.5),
        "w_ff1": nrm(ks[15], (DEPTH, D_MODEL, D_FF), D_MODEL ** -0.5),
        "w_ff2": nrm(ks[16], (DEPTH, D_FF, D_MODEL), D_FF ** -0.5),
        "final_norm_w": 1.0 + nrm(ks[17], (D_MODEL,), 0.02),
    }


def reference(x, c, positions, ada_w, ada_b, norm_mix_w, norm_mlp_w, w_in, conv_w, conv_b,
              fox_f_bias, mlstm_i_bias, mlstm_f_bias, merge_scale, w_out, w_ff1, w_ff2, final_norm_w):
    cond = jax.nn.silu(c)
    for layer in range(DEPTH):
        mod = (cond @ ada_w[layer] + ada_b[layer])[:, None, :]
        sh_a, sc_a, g_a, sh_m, sc_m, g_m = jnp.split(mod, N_MOD, axis=-1)
        h = rmsnorm(x, norm_mix_w[layer]) * (1.0 + sc_a) + sh_a
        x = x + g_a * hybrid_mixer(h, positions, w_in[layer], conv_w[layer], conv_b[layer],
                                   fox_f_bias[layer], mlstm_i_bias[layer], mlstm_f_bias[layer],
                                   merge_scale[layer], w_out[layer])
        h = rmsnorm(x, norm_mlp_w[layer]) * (1.0 + sc_m) + sh_m
        x = x + g_m * (jnp.square(jax.nn.relu(h @ w_ff1[layer])) @ w_ff2[layer])
    return rmsnorm(x, final_norm_w)
```

```python
import math
import ml_dtypes
import contextlib
import numpy as np
import concourse.bass as bass
import concourse.mybir as mybir
from concourse.bass_utils import run_bass_kernel_spmd

F32 = mybir.dt.float32
BF16 = mybir.dt.bfloat16
I32 = mybir.dt.int32
AF = mybir.ActivationFunctionType
ALU = mybir.AluOpType
AX = mybir.AxisListType

EPOCH = 30000
ENGS = ["pe", "act", "dve", "pool", "sp"]


class Buf:
    def __init__(self, t, name):
        self.t = t
        self.name = name
        self.last_write = None
        self.reads = {}
        self.dma_sem = None
        self.dma_cnt = 0

    def __getitem__(self, idx):
        return self.t[idx]


class Op:
    __slots__ = ("eng", "fn", "deps", "sig", "sig_idx", "dma_buf", "dma_val", "n_dma")

    def __init__(self, eng, fn):
        self.eng = eng
        self.fn = fn
        self.deps = []
        self.sig = False
        self.sig_idx = -1
        self.dma_buf = None
        self.dma_val = 0
        self.n_dma = 0


class Prog:
    def __init__(self, nc):
        self.nc = nc
        self.ops = {e: [] for e in ENGS}
        self.stack = contextlib.ExitStack()
        self.bufs = []
        self.nbuf = 0

    def sbuf(self, name, shape, dtype):
        t = self.stack.enter_context(self.nc.sbuf_tensor(name, list(shape), dtype))
        b = Buf(t, name)
        self.bufs.append(b)
        return b

    def psum(self, name, shape, dtype=F32):
        t = self.stack.enter_context(self.nc.psum_tensor(name, list(shape), dtype))
        b = Buf(t, name)
        self.bufs.append(b)
        return b

    def alias(self, t, name):
        b = Buf(t, name)
        self.bufs.append(b)
        return b

    def _track(self, op, reads, writes):
        deps = []
        for b in reads:
            if b.last_write is not None:
                deps.append(b.last_write)
        for b in writes:
            if b.last_write is not None:
                deps.append(b.last_write)
            deps.extend(b.reads.values())
        seen = set()
        for d in deps:
            if d is op or id(d) in seen:
                continue
            if d.eng == "pe" and op.eng == "pe" and d.dma_buf is None:
                continue
            seen.add(id(d))
            op.deps.append(d)
            d.sig = True
        for b in reads:
            if b in writes:
                continue
            key = op.eng if op.dma_buf is None else ("d", id(op))
            b.reads[key] = op
        for b in writes:
            b.last_write = op
            b.reads = {}

    def op(self, eng, fn, reads=(), writes=()):
        o = Op(eng, fn)
        self._track(o, reads, writes)
        self.ops[eng].append(o)
        return o

    def dma(self, eng, fn, sb, n=1, reads=(), writes=()):
        o = Op(eng, fn)
        o.dma_buf = sb
        o.n_dma = n
        sb.dma_cnt += 16 * n
        o.dma_val = sb.dma_cnt
        o.sig = True
        self._track(o, reads, list(writes) + ([sb] if sb not in writes else []))
        self.ops[eng].append(o)
        return o

    def emit(self, final_waits=()):
        nc = self.nc
        nsig = {}
        for e in ENGS:
            k = 0
            for o in self.ops[e]:
                if o.dma_buf is None and o.sig:
                    o.sig_idx = k
                    k += 1
            nsig[e] = k
        sems = {}
        for e in ENGS:
            for ep in range((nsig[e] + EPOCH - 1) // EPOCH):
                sems[(e, ep)] = self.stack.enter_context(nc.semaphore(f"s_{e}_{ep}"))
        for b in self.bufs:
            if b.dma_cnt > 0:
                b.dma_sem = self.stack.enter_context(nc.semaphore(f"d_{b.name}"))

        def token(o):
            if o.dma_buf is not None:
                return o.dma_buf.dma_sem, ("d", id(o.dma_buf)), o.dma_val
            ep = o.sig_idx // EPOCH
            return sems[(o.eng, ep)], (o.eng, ep), o.sig_idx % EPOCH + 1

        def run_engine(ename, engine):
            seen = {}
            for o in self.ops[ename]:
                for d in o.deps:
                    sem, key, val = token(d)
                    if seen.get(key, 0) >= val:
                        continue
                    seen[key] = val
                    engine.wait_ge(sem, val)
                r = o.fn(engine)
                if o.dma_buf is not None:
                    insts = r if isinstance(r, (list, tuple)) else [r]
                    assert len(insts) == o.n_dma, (len(insts), o.n_dma)
                    for i in insts:
                        i.then_inc(o.dma_buf.dma_sem, 16)
                elif o.sig:
                    sem, key, val = token(o)
                    r.then_inc(sem, 1)
            for o in final_waits:
                if o.eng == ename or True:
                    pass

        with nc.Block() as block:
            @block.tensor
            def _(eng):
                run_engine("pe", eng)

            @block.scalar
            def _(eng):
                run_engine("act", eng)

            @block.vector
            def _(eng):
                run_engine("dve", eng)

            @block.gpsimd
            def _(eng):
                run_engine("pool", eng)

            @block.sync
            def _(eng):
                run_engine("sp", eng)
                for o in final_waits:
                    sem, key, val = token(o)
                    eng.wait_ge(sem, val)
        self.stack.close()


D = 2048
DC = 16
DFF = 8192
FC = 64
TT = 512
EPS = 1e-6


def _b(x):
    return x.buf if isinstance(x, _Col) else x


def norm_mod(P, xs, hs, sq, ones_bf, ps_ss, rstd, tmp, A, Bc, sink=None):
    for dc in range(DC):
        P.op("act", lambda e, dc=dc: e.activation(out=sq[:, dc, :], in_=xs[:, dc, :], func=AF.Square),
             reads=[xs], writes=[sq])
    for dc in range(DC):
        P.op("pe", lambda e, dc=dc: e.matmul(ps_ss[:, :], ones_bf[:, :], sq[:, dc, :], start=(dc == 0), stop=(dc == DC - 1)),
             reads=[ones_bf, sq], writes=[ps_ss])
    P.op("dve", lambda e: e.tensor_scalar(out=rstd[:, :], in0=ps_ss[:, :], scalar1=1.0 / D, scalar2=EPS, op0=ALU.mult, op1=ALU.add),
         reads=[ps_ss], writes=[rstd])
    P.op("act", lambda e: e.activation(out=rstd[:, :], in_=rstd[:, :], func=AF.Sqrt), reads=[rstd], writes=[rstd])
    P.op("dve", lambda e: e.reciprocal(out=rstd[:, :], in_=rstd[:, :]), reads=[rstd], writes=[rstd])
    for dc in range(DC):
        t = tmp[dc % 2]
        P.op("dve", lambda e, dc=dc, t=t: e.scalar_tensor_tensor(out=t[:, :], in0=xs[:, dc, :], scalar=A[:, dc:dc + 1], in1=rstd[:, :],
                                                               op0=ALU.mult, op1=ALU.mult),
             reads=[xs, _b(A), rstd], writes=[t])
        if sink is not None:
            sink(dc, t)
        elif Bc is not None:
            P.op("act", lambda e, dc=dc, t=t: e.activation(out=hs[:, dc, :], in_=t[:, :], func=AF.Identity, bias=Bc[:, dc:dc + 1]),
                 reads=[t, _b(Bc)], writes=[hs])


def build_F(ntok, has_mlp, final):
    nc = bass.Bass("TRN2", target_bir_lowering=False)
    ntile = ntok // TT
    xT = nc.dram_tensor("xT", [D, ntok], F32, kind="ExternalInput").ap()
    vec = nc.dram_tensor("vec", [128, 10, DC], F32, kind="ExternalInput").ap()
    if has_mlp:
        yT = nc.dram_tensor("yT", [D, ntok], BF16, kind="ExternalInput").ap()
        w_out = nc.dram_tensor("w_out", [D, D], F32, kind="ExternalInput").ap()
        w_ff1 = nc.dram_tensor("w_ff1", [D, DFF], F32, kind="ExternalInput").ap()
        w_ff2 = nc.dram_tensor("w_ff2", [DFF, D], F32, kind="ExternalInput").ap()
        if not final:
            x_out = nc.dram_tensor("x_out", [D, ntok], F32, kind="ExternalOutput").ap()
    if final:
        h_out = nc.dram_tensor("h_out", [D, ntok], F32, kind="ExternalOutput").ap()
    else:
        h_out = nc.dram_tensor("h_out", [D, ntok], BF16, kind="ExternalOutput").ap()

    P = Prog(nc)
    xs = P.sbuf("xs", [128, DC, TT], F32)
    vecs = P.sbuf("vecs", [128, 10, DC], F32)
    A_mlp = P.sbuf("A_mlp", [128, DC], F32)
    A_nxt = P.sbuf("A_nxt", [128, DC], F32)
    ones_bf = P.sbuf("ones_bf", [128, 128], BF16)
    rstd = P.sbuf("rstd", [128, TT], F32)
    tmp = [P.sbuf(f"tmp{i}", [128, TT], F32) for i in range(2)]
    ps = [P.psum(f"ps{i}", [128, TT], F32) for i in range(8)]
    ps_ss = ps[7]
    if has_mlp:
        ys = [P.sbuf(f"ys{i}", [128, DC, TT], BF16) for i in range(1)]
        hm = P.sbuf("hm", [128, DC, TT], BF16)
        hs = hm
        us = P.sbuf("us", [128, FC, TT], BF16)
        sq = us
        NW = 4
        wr = [P.sbuf(f"wr{i}", [128, 16, 256], BF16) for i in range(NW)]
        rl = [P.sbuf(f"rl{i}", [128, TT], BF16) for i in range(2)]
    else:
        sq = P.sbuf("sq", [128, DC, TT], BF16)
        hs = P.sbuf("hs", [128, DC, TT], BF16)

    P.dma("sp", lambda e: [e.dma_start(out=vecs[:, :, :], in_=vec)], vecs, 1)
    P.op("pool", lambda e: e.memset(ones_bf[:, :], 1.0), writes=[ones_bf])
    P.op("dve", lambda e: e.scalar_tensor_tensor(out=A_mlp[:, :], in0=vecs[:, 2, :], scalar=1.0, in1=vecs[:, 1, :], op0=ALU.add, op1=ALU.mult),
         reads=[vecs], writes=[A_mlp])
    P.op("dve", lambda e: e.scalar_tensor_tensor(out=A_nxt[:, :], in0=vecs[:, 6, :], scalar=1.0, in1=vecs[:, 5, :], op0=ALU.add, op1=ALU.mult),
         reads=[vecs], writes=[A_nxt])
    g_a = vecs
    wctr = [0]
    psctr = [0]

    def wload(src_ap):
        b = wr[wctr[0] % NW]
        wctr[0] += 1
        P.dma("pool", lambda e, b=b, src_ap=src_ap: [e.dma_start(out=b[:, :, :], in_=src_ap)], b, 1)
        return b

    def next_ps():
        b = ps[psctr[0] % 6]
        psctr[0] += 1
        return b

    outs = []
    for ti in range(ntile):
        t0 = ti * TT
        P.dma("sp", lambda e, t0=t0: [e.dma_start(out=xs[:, :, :], in_=xT[:, t0:t0 + TT].rearrange("(c p) t -> p c t", p=128))], xs, 1)
        if has_mlp:
            yb = ys[0]
            P.dma("sp", lambda e, t0=t0, yb=yb: [e.dma_start(out=yb[:, :, :], in_=yT[:, t0:t0 + TT].rearrange("(c p) t -> p c t", p=128))], yb, 1)
            for cb in range(D // 256):
                wb = wload(w_out[:, cb * 256:(cb + 1) * 256].rearrange("(c p) f -> p c f", p=128))
                for j in range(2):
                    dc = cb * 2 + j
                    pb = next_ps()
                    for k in range(DC):
                        P.op("pe", lambda e, k=k, j=j, wb=wb, pb=pb, yb=yb: e.matmul(pb[:, :], wb[:, k, j * 128:(j + 1) * 128], yb[:, k, :],
                                                                                  start=(k == 0), stop=(k == DC - 1)),
                             reads=[wb, yb], writes=[pb])
                    P.op("dve", lambda e, dc=dc, pb=pb: e.scalar_tensor_tensor(out=xs[:, dc, :], in0=pb[:, :], scalar=vecs[:, 0, dc:dc + 1], in1=xs[:, dc, :],
                                                                              op0=ALU.mult, op1=ALU.add),
                         reads=[pb, vecs, xs], writes=[xs])
            norm_mod(P, xs, hm, sq, ones_bf, ps_ss, rstd, tmp, A_mlp, _Col(vecs, 3))
            for cb in range(DFF // 256):
                wb = wload(w_ff1[:, cb * 256:(cb + 1) * 256].rearrange("(c p) f -> p c f", p=128))
                for j in range(2):
                    fc = cb * 2 + j
                    pb = next_ps()
                    for k in range(DC):
                        P.op("pe", lambda e, k=k, j=j, wb=wb, pb=pb: e.matmul(pb[:, :], wb[:, k, j * 128:(j + 1) * 128], hm[:, k, :],
                                                                           start=(k == 0), stop=(k == DC - 1)),
                             reads=[wb, hm], writes=[pb])
                    r = rl[fc % 2]
                    P.op("act", lambda e, pb=pb, r=r: e.activation(out=r[:, :], in_=pb[:, :], func=AF.Relu), reads=[pb], writes=[r])
                    P.op("pool", lambda e, fc=fc, r=r: e.tensor_tensor(out=us[:, fc, :], in0=r[:, :], in1=r[:, :], op=ALU.mult), reads=[r], writes=[us])
            for cb in range(D // 256):
                pbs = [next_ps(), next_ps()]
                for kg in range(FC // 16):
                    wb = wload(w_ff2[kg * 2048:(kg + 1) * 2048, cb * 256:(cb + 1) * 256].rearrange("(c p) f -> p c f", p=128))
                    for j in range(2):
                        for kk in range(16):
                            k = kg * 16 + kk
                            P.op("pe", lambda e, k=k, kk=kk, j=j, wb=wb, pb=pbs[j]: e.matmul(pb[:, :], wb[:, kk, j * 128:(j + 1) * 128], us[:, k, :],
                                                                                        start=(k == 0), stop=(k == FC - 1)),
                                 reads=[wb, us], writes=[pbs[j]])
                for j in range(2):
                    dc = cb * 2 + j
                    P.op("dve", lambda e, dc=dc, pb=pbs[j]: e.scalar_tensor_tensor(out=xs[:, dc, :], in0=pb[:, :], scalar=vecs[:, 4, dc:dc + 1], in1=xs[:, dc, :],
                                                                                  op0=ALU.mult, op1=ALU.add),
                         reads=[pbs[j], vecs, xs], writes=[xs])
            if not final:
                outs.append(P.dma("sp", lambda e, t0=t0: [e.dma_start(out=x_out[:, t0:t0 + TT].rearrange("(c p) t -> p c t", p=128), in_=xs[:, :, :])], xs, 1))
        if final:
            def sink(dc, t, t0=t0):
                outs.append(P.dma("sp", lambda e: [e.dma_start(out=h_out[dc * 128:(dc + 1) * 128, t0:t0 + TT], in_=t[:, :])], t, 1))
            norm_mod(P, xs, None, sq, ones_bf, ps_ss, rstd, tmp, _Col(vecs, 5), None, sink=sink)
        else:
            norm_mod(P, xs, hs, sq, ones_bf, ps_ss, rstd, tmp, A_nxt, _Col(vecs, 7))
            outs.append(P.dma("sp", lambda e, t0=t0: [e.dma_start(out=h_out[:, t0:t0 + TT].rearrange("(c p) t -> p c t", p=128), in_=hs[:, :, :])], hs, 1))
    P.emit(final_waits=outs)
    return nc


class _Col:
    def __init__(self, buf, row):
        self.buf = buf
        self.row = row

    def __getitem__(self, idx):
        p, c = idx
        return self.buf.t[p, self.row, c]


HD = 128
SCL = HD ** -0.5
TWO_PI = 2.0 * math.pi
CW_HI = 6.28125
CW_LO = TWO_PI - 6.28125
C_FOX = [0, 385]
C_RET = 770
C_ML = 1538
NCOL = 2052
NPRM = 24


def build_M(S):
    nc = bass.Bass("TRN2", target_bir_lowering=False)
    NT = S // 512
    NB = S // 128
    hT = nc.dram_tensor("hT", [2048, S], BF16, kind="ExternalInput").ap()
    posr = nc.dram_tensor("posr", [128, S], I32, kind="ExternalInput").ap()
    Wc = nc.dram_tensor("Wc", [2048, NCOL], F32, kind="ExternalInput").ap()
    prm = nc.dram_tensor("prm", [128, NPRM], F32, kind="ExternalInput").ap()
    cst = nc.dram_tensor("cst", [128, 4, 128], F32, kind="ExternalInput").ap()
    yT = nc.dram_tensor("yT", [512, S], BF16, kind="ExternalOutput").ap()

    P = Prog(nc)
    outs = []

    def mm(ob, oap, lb, lap, rb, rap, start=True, stop=True):
        P.op("pe", lambda e: e.matmul(oap, lap, rap, start=start, stop=stop), reads=[lb, rb], writes=[ob])

    def act(ob, oap, ib, iap, func, bias=None, scale=None, rd=()):
        kw = {}
        if bias is not None:
            kw["bias"] = bias
        if scale is not None:
            kw["scale"] = scale
        P.op("act", lambda e: e.activation(out=oap, in_=iap, func=func, **kw), reads=[ib] + list(rd), writes=[ob])

    def ts(ob, oap, ib, iap, s1, s2, op0, op1=None, rd=(), eng="dve"):
        if op1 is None:
            P.op(eng, lambda e: e.tensor_scalar(out=oap, in0=iap, scalar1=s1, scalar2=None, op0=op0), reads=[ib] + list(rd), writes=[ob])
        else:
            P.op(eng, lambda e: e.tensor_scalar(out=oap, in0=iap, scalar1=s1, scalar2=s2, op0=op0, op1=op1), reads=[ib] + list(rd), writes=[ob])

    def tt(ob, oap, ab, aap, bb, bap, op, eng="dve"):
        P.op(eng, lambda e: e.tensor_tensor(out=oap, in0=aap, in1=bap, op=op), reads=[ab, bb], writes=[ob])

    def stt(ob, oap, ab, aap, sc, bb, bap, op0, op1, rd=(), eng="dve"):
        P.op(eng, lambda e: e.scalar_tensor_tensor(out=oap, in0=aap, scalar=sc, in1=bap, op0=op0, op1=op1),
             reads=[ab, bb] + list(rd), writes=[ob])

    def cp(ob, oap, ib, iap, eng="dve"):
        P.op(eng, lambda e: e.tensor_copy(out=oap, in_=iap), reads=[ib], writes=[ob])

    def rsum(ob, oap, ib, iap):
        P.op("dve", lambda e: e.reduce_sum(out=oap, in_=iap, axis=AX.X), reads=[ib], writes=[ob])

    hs = [P.sbuf(f"hs{i}", [128, 16, 512], BF16) for i in range(2)]
    cf = P.sbuf("cf", [128, 4, 128], F32)
    cb = P.sbuf("cb", [128, 4, 128], BF16)
    pr = P.sbuf("pr", [128, NPRM + 8], F32)
    yt = [P.sbuf(f"yt{i}", [128, 512], BF16) for i in range(2)]
    st = P.sbuf("st", [128, 8], F32)
    rs = P.sbuf("rs", [128, 2], F32)
    ycb = P.sbuf("ycb", [128, 128], F32)
    sqj = P.sbuf("sqj", [128, 128], F32)
    ynb = P.sbuf("ynb", [128, 128], BF16)
    yf = [P.sbuf(f"yf{i}", [128, 128], F32) for i in range(2)]
    psb = [P.psum(f"psb{i}", [128, 512], F32) for i in range(7)]
    pst = P.psum("pst", [128, 1024], BF16)

    P.dma("sp", lambda e: [e.dma_start(out=cf[:, :, :], in_=cst)], cf, 1)
    P.dma("sp", lambda e: [e.dma_start(out=pr[:, 0:NPRM], in_=prm)], pr, 1)
    cp(cb, cb[:, :, :], cf, cf[:, :, :])
    ts(pr, pr[:, 24:26], pr, pr[:, 0:2], -1.0, None, ALU.mult)
    ts(pr, pr[:, 26:27], pr, pr[:, 3:4], -1.0, None, ALU.mult)
    P.op("dve", lambda e: e.memset(pr[:, 27:28], -math.pi), reads=[], writes=[pr])
    U_f = cf[:, 0, :]
    ones_f = cf[:, 3, :]
    mask01_f = cf[:, 0, :]
    ident_b = cb[:, 2, :]
    maskb_b = cb[:, 1, :]

    hload_ctr = [0]

    def hload(ti):
        b = hs[hload_ctr[0] % 2]
        hload_ctr[0] += 1
        P.dma("sp", lambda e: [e.dma_start(out=b[:, :, :], in_=hT[:, ti * 512:(ti + 1) * 512].rearrange("(c p) t -> p c t", p=128))], b, 1)
        return b

    def wload(wb, c0, ncols):
        P.dma("pool", lambda e: [e.dma_start(out=wb[:, :, 0:ncols], in_=Wc[:, c0:c0 + ncols].rearrange("(c p) f -> p c f", p=128))], wb, 1)

    pctr = [0]

    def proj_bank():
        b = psb[pctr[0] % 2]
        pctr[0] += 1
        return b

    def proj_fm(hb, wb, c0):
        pb = proj_bank()
        for k in range(16):
            mm(pb, pb[:, :], wb, wb[:, k, c0:c0 + 128], hb, hb[:, k, :], start=(k == 0), stop=(k == 15))
        return pb

    def proj_tm(hb, j, wb, c0, n):
        pb = proj_bank()
        for k in range(16):
            mm(pb, pb[:, 0:n], hb, hb[:, k, j * 128:(j + 1) * 128], wb, wb[:, k, c0:c0 + n], start=(k == 0), stop=(k == 15))
        return pb

    yt_ctr = [0]

    def finish_block(yb, yap, ms_col, gate, ytb, j):
        rsum(st, st[:, 0:1], yb, yap)
        ts(st, st[:, 1:2], st, st[:, 0:1], -1.0 / HD, None, ALU.mult)
        ts(ycb, ycb[:, :], yb, yap, st[:, 1:2], None, ALU.add, rd=[st])
        tt(sqj, sqj[:, :], ycb, ycb[:, :], ycb, ycb[:, :], ALU.mult)
        rsum(st, st[:, 2:3], sqj, sqj[:, :])
        ts(st, st[:, 3:4], st, st[:, 2:3], 1.0 / HD, 1e-6, ALU.mult, ALU.add)
        act(rs, rs[:, 1:2], st, st[:, 3:4], AF.Ln)
        act(rs, rs[:, 0:1], rs, rs[:, 1:2], AF.Exp, scale=-0.5)
        ts(ynb, ynb[:, :], ycb, ycb[:, :], rs[:, 0:1], None, ALU.mult, rd=[rs])
        P.op("pe", lambda e: e.transpose(out=pst[:, 0:128], in_=ynb[:, :], identity=ident_b), reads=[ynb, cb], writes=[pst])
        if gate is None:
            ts(ytb, ytb[:, j * 128:(j + 1) * 128], pst, pst[:, 0:128], ms_col, None, ALU.mult, rd=[pr])
        else:
            gb, gap = gate
            stt(ytb, ytb[:, j * 128:(j + 1) * 128], pst, pst[:, 0:128], ms_col, gb, gap, ALU.mult, ALU.mult, rd=[pr])

    def store_y(ytb, slot, ti):
        outs.append(P.dma("sp", lambda e: [e.dma_start(out=yT[slot * 128:(slot + 1) * 128, ti * 512:(ti + 1) * 512], in_=ytb[:, :])], ytb, 1))

    wfx = P.sbuf("wfx", [128, 16, 388], BF16)
    kT_t = [P.sbuf(f"kT{i}", [128, 512], BF16) for i in range(NT)]
    v_t = [P.sbuf(f"v{i}", [128, 4, 129], BF16) for i in range(NT)]
    nb_all = P.sbuf("nb_all", [128, NB], F32)
    biasm = [P.sbuf(f"biasm{i}", [128, NB], F32) for i in range(2)]
    qs = [P.sbuf(f"qs{i}", [128, 512], BF16) for i in range(2)]
    fcol = P.sbuf("fcol", [128, 8], F32)
    gl = P.sbuf("gl", [128, 16], F32)
    carry = P.sbuf("carry", [128, 2], F32)
    E_r = [P.sbuf(f"E{i}", [128, 128], BF16) for i in range(3)]
    ST_r = psb[2:5]
    O_r = psb[5:7]

    for fh in range(2):
        slot = 1 + fh
        wload(wfx, C_FOX[fh], 385)
        for ti in range(NT):
            P.op("pool", lambda e, ti=ti: e.memset(v_t[ti][:, :, 128:129], 1.0), writes=[v_t[ti]])
        P.op("pool", lambda e: e.memset(carry[:, :], 0.0), writes=[carry])
        negb = pr[:, 24 + fh:25 + fh]
        ms_col = pr[:, 15 + fh:16 + fh]
        hb_next = hload(0)
        bctr = 0
        octr = 0
        for ti in range(NT):
            hb = hb_next
            pq = proj_fm(hb, wfx, 0)
            qb_ = qs[ti % 2]
            ts(qb_, qb_[:, :], pq, pq[:, :], SCL, None, ALU.mult)
            pk = proj_fm(hb, wfx, 128)
            cp(kT_t[ti], kT_t[ti][:, :], pk, pk[:, :])
            for j in range(4):
                pv = proj_tm(hb, j, wfx, 256, 129)
                cp(v_t[ti], v_t[ti][:, j, 0:128], pv, pv[:, 0:128])
                cp(fcol, fcol[:, j:j + 1], pv, pv[:, 128:129])
            if ti + 1 < NT:
                hb_next = hload(ti + 1)
            act(gl, gl[:, 0:4], fcol, fcol[:, 0:4], AF.Exp, bias=negb, scale=-1.0, rd=[pr])
            act(gl, gl[:, 4:8], gl, gl[:, 0:4], AF.Ln, bias=1.0)
            pg = proj_bank()
            mm(pg, pg[:, 0:4], cf, U_f, gl, gl[:, 4:8])
            mm(pg, pg[:, 4:8], cf, ones_f, gl, gl[:, 4:8])
            cp(gl, gl[:, 8:9], carry, carry[:, 0:1])
            for j in range(1, 4):
                tt(gl, gl[:, 8 + j:9 + j], gl, gl[:, 7 + j:8 + j], pg, pg[:, 3 + j:4 + j], ALU.add)
            tt(carry, carry[:, 0:1], gl, gl[:, 11:12], pg, pg[:, 7:8], ALU.add)
            tt(nb_all, nb_all[:, ti * 4:ti * 4 + 4], gl, gl[:, 8:12], pg, pg[:, 0:4], ALU.add)
            pairs = []
            for j in range(4):
                qb = ti * 4 + j
                for kb in range(qb + 1):
                    pairs.append((j, qb, kb))
            LA = 2
            cur = {}
            n = len(pairs)
            for step in range(n + LA):
                if step < n:
                    j, qb, kb = pairs[step]
                    if kb == 0:
                        bm = biasm[bctr % 2]
                        bctr += 1
                        ts(bm, bm[:, 0:qb + 1], nb_all, nb_all[:, 0:qb + 1], gl[:, 8 + j:9 + j], None, ALU.subtract, rd=[gl])
                        cur[j] = bm
                    bm = cur[j]
                    stb = ST_r[step % 3]
                    eb = E_r[step % 3]
                    kt = kT_t[kb // 4]
                    diag = (kb == qb)
                    mm(stb, stb[:, 0:128], kt, kt[:, (kb % 4) * 128:(kb % 4 + 1) * 128], qb_, qb_[:, j * 128:(j + 1) * 128],
                       start=True, stop=not diag)
                    if diag:
                        mm(stb, stb[:, 0:128], cb, ident_b, cb, maskb_b, start=False, stop=True)
                    act(eb, eb[:, :], stb, stb[:, 0:128], AF.Exp, bias=bm[:, kb:kb + 1], rd=[bm])
                if step >= LA:
                    j, qb, kb = pairs[step - LA]
                    eb = E_r[(step - LA) % 3]
                    if kb == 0:
                        ob = O_r[octr % 2]
                        octr += 1
                        cur[("o", j)] = ob
                    ob = cur[("o", j)]
                    vt = v_t[kb // 4]
                    mm(ob, ob[:, 0:129], eb, eb[:, :], vt, vt[:, kb % 4, :], start=(kb == 0), stop=(kb == qb))
                    if kb == qb:
                        P.op("dve", lambda e, ob=ob: e.reciprocal(out=st[:, 5:6], in_=ob[:, 128:129]), reads=[ob], writes=[st])
                        yb = yf[j % 2]
                        ts(yb, yb[:, :], ob, ob[:, 0:128], st[:, 5:6], None, ALU.mult, rd=[st])
                        ytb = yt[ti % 2]
                        finish_block(yb, yb[:, :], ms_col, None, ytb, j)
            store_y(yt[ti % 2], slot, ti)

    wl = P.sbuf("wl", [128, 16, 768], BF16)
    Z = P.sbuf("Z", [128, 129], F32)
    Zg = P.sbuf("Zg", [128, 129], F32)
    Zb = P.sbuf("Zb", [128, 129], BF16)
    qT = P.sbuf("qTl", [128, 512], BF16)
    kTl = P.sbuf("kTl", [128, 512], BF16)
    ktm = P.sbuf("ktm", [128, 4, 128], BF16)
    vaug = [P.sbuf(f"vaug{i}", [128, 129], BF16) for i in range(2)]
    vraw = P.sbuf("vraw", [128, 4, 130], F32)
    gateT = P.sbuf("gateT", [128, 512], BF16)
    PT = P.sbuf("PT", [128, 128], BF16)
    tA = P.sbuf("tA", [128, 512], F32)
    tB = P.sbuf("tB", [128, 512], F32)
    posf = P.sbuf("posf", [128, 512], F32)
    posi = P.sbuf("posi", [128, 512], I32)
    qi = P.sbuf("qi", [128, 512], I32)
    sn = P.sbuf("sn", [128, 512], F32)
    cs = P.sbuf("cs", [128, 512], F32)
    cin = [P.sbuf(f"cin{i}", [128, 515], F32) for i in range(2)]
    gm = P.sbuf("gm", [128, 32], F32)
    ps_st, ps_o, ps_u, ps_g = psb[2], psb[3], psb[4], psb[5]

    for lp_ in range(2):
        is_ml = (lp_ == 1)
        slot = 3 if is_ml else 0
        ms_col = pr[:, 17:18] if is_ml else pr[:, 14:15]
        if is_ml:
            wload(wl, C_ML, 514)
        else:
            wload(wl, C_RET, 768)
        P.op("pool", lambda e: e.memset(Z[:, :], 0.0), writes=[Z])
        P.op("pool", lambda e: e.memset(Zb[:, :], 0.0), writes=[Zb])
        if is_ml:
            for i in range(2):
                P.op("pool", lambda e, i=i: e.memset(cin[i][:, 0:3], 0.0), writes=[cin[i]])
        hb_next = hload(0)
        for ti in range(NT):
            hb = hb_next
            if not is_ml:
                P.dma("sp", lambda e, ti=ti: [e.dma_start(out=posi[:, :], in_=posr[:, ti * 512:(ti + 1) * 512])], posi, 1)
                cp(posf, posf[:, :], posi, posi[:, :])
                ts(posf, posf[:, :], posf, posf[:, :], pr[:, 21:22], None, ALU.mult, rd=[pr])
                ts(tA, tA[:, :], posf, posf[:, :], 1.0 / TWO_PI, None, ALU.mult)
                cp(qi, qi[:, :], tA, tA[:, :])
                cp(tA, tA[:, :], qi, qi[:, :])
                stt(tB, tB[:, :], tA, tA[:, :], -CW_HI, posf, posf[:, :], ALU.mult, ALU.add)
                stt(tB, tB[:, :], tA, tA[:, :], -CW_LO, tB, tB[:, :], ALU.mult, ALU.add)
                ts(tA, tA[:, :], tB, tB[:, :], math.pi, -TWO_PI, ALU.is_gt, ALU.mult)
                tt(sn, sn[:, :], tB, tB[:, :], tA, tA[:, :], ALU.add)
                ts(cs, cs[:, :], sn, sn[:, :], 0.5 * math.pi, None, ALU.add)
                ts(tA, tA[:, :], cs, cs[:, :], math.pi, -TWO_PI, ALU.is_gt, ALU.mult)
                tt(cs, cs[:, :], cs, cs[:, :], tA, tA[:, :], ALU.add)
                ts(sn, sn[:, :], sn, sn[:, :], -3.14159, 3.14159, ALU.max, ALU.min)
                ts(cs, cs[:, :], cs, cs[:, :], -3.14159, 3.14159, ALU.max, ALU.min)
                act(sn, sn[:, :], sn, sn[:, :], AF.Sin)
                act(cs, cs[:, :], cs, cs[:, :], AF.Sin)
                for which, (c_plain, c_perm, dst, sg, scl) in enumerate([(0, 128, qT, pr[:, 22:23], 1.0),
                                                                         (256, 384, kTl, pr[:, 23:24], SCL)]):
                    pp = proj_fm(hb, wl, c_perm)
                    tt(tA, tA[:, :], pp, pp[:, :], sn, sn[:, :], ALU.mult)
                    pq = proj_fm(hb, wl, c_plain)
                    stt(tB, tB[:, :], pq, pq[:, :], scl, cs, cs[:, :], ALU.mult, ALU.mult)
                    stt(dst, dst[:, :], tA, tA[:, :], sg, tB, tB[:, :], ALU.mult, ALU.add, rd=[pr])
                pgt = proj_fm(hb, wl, 640)
                act(gateT, gateT[:, :], pgt, pgt[:, :], AF.Silu)
                vcol, vn = 512, 128
            else:
                for which, (c0, dst, wc, bc, scl) in enumerate([(0, qT, 4, 12, None), (128, kTl, 8, 13, SCL)]):
                    ci = cin[which]
                    pq = proj_fm(hb, wl, c0)
                    cp(ci, ci[:, 3:515], pq, pq[:, :])
                    ts(tA, tA[:, :], ci, ci[:, 0:512], pr[:, wc:wc + 1], pr[:, bc:bc + 1], ALU.mult, ALU.add, rd=[pr])
                    for jj in range(1, 4):
                        stt(tA, tA[:, :], ci, ci[:, jj:jj + 512], pr[:, wc + jj:wc + jj + 1], tA, tA[:, :], ALU.mult, ALU.add, rd=[pr])
                    cp(ci, ci[:, 0:3], ci, ci[:, 512:515])
                    if scl is None:
                        act(dst, dst[:, :], tA, tA[:, :], AF.Silu)
                    else:
                        act(tB, tB[:, :], tA, tA[:, :], AF.Silu)
                        ts(dst, dst[:, :], tB, tB[:, :], scl, None, ALU.mult)
                pgt = proj_fm(hb, wl, 386)
                act(gateT, gateT[:, :], pgt, pgt[:, :], AF.Sigmoid)
                vcol, vn = 256, 130
            for j in range(4):
                P.op("pe", lambda e, j=j: e.transpose(out=pst[:, 128 + j * 128:256 + j * 128], in_=kTl[:, j * 128:(j + 1) * 128], identity=ident_b),
                     reads=[kTl, cb], writes=[pst])
            cp(ktm, ktm[:, :, :], pst, pst[:, 128:640].rearrange("p (j d) -> p j d", j=4))
            for j in range(4):
                pv = proj_tm(hb, j, wl, vcol, vn)
                cp(vraw, vraw[:, j, 0:vn], pv, pv[:, 0:vn])
            if ti + 1 < NT:
                hb_next = hload(ti + 1)
            if is_ml:
                act(gm, gm[:, 0:4], vraw, vraw[:, :, 129], AF.Exp, bias=pr[:, 26:27], scale=-1.0, rd=[pr])
                act(gm, gm[:, 4:8], gm, gm[:, 0:4], AF.Ln, bias=1.0)
                mm(ps_g, ps_g[:, 0:4], cf, U_f, gm, gm[:, 4:8])
                mm(ps_g, ps_g[:, 4:8], cf, ones_f, gm, gm[:, 4:8])
                act(gm, gm[:, 12:16], ps_g, ps_g[:, 0:4], AF.Exp, scale=-1.0)
                act(gm, gm[:, 24:28], ps_g, ps_g[:, 4:8], AF.Exp, scale=-1.0)
                tt(gm, gm[:, 16:20], vraw, vraw[:, :, 128], ps_g, ps_g[:, 0:4], ALU.add)
                act(gm, gm[:, 20:24], gm, gm[:, 16:20], AF.Exp, bias=pr[:, 2:3], rd=[pr])
            for j in range(4):
                va = vaug[j % 2]
                if is_ml:
                    c_col = gm[:, 20 + j:21 + j]
                    g_col = gm[:, 24 + j:25 + j]
                    cbuf = gm
                else:
                    c_col = pr[:, 18:19]
                    g_col = pr[:, 20:21]
                    cbuf = pr
                ts(va, va[:, 0:128], vraw, vraw[:, j, 0:128], c_col, None, ALU.mult, rd=[cbuf])
                cp(va, va[:, 128:129], cbuf, c_col)
                mm(ps_st, ps_st[:, 0:128], kTl, kTl[:, j * 128:(j + 1) * 128], qT, qT[:, j * 128:(j + 1) * 128])
                tt(PT, PT[:, :], ps_st, ps_st[:, 0:128], cf, mask01_f, ALU.mult)
                mm(ps_o, ps_o[:, 0:129], PT, PT[:, :], va, va[:, :], start=True, stop=False)
                mm(ps_o, ps_o[:, 0:129], qT, qT[:, j * 128:(j + 1) * 128], Zb, Zb[:, :], start=False, stop=True)
                mm(ps_u, ps_u[:, 0:129], ktm, ktm[:, j, :], va, va[:, :])
                ts(Zg, Zg[:, :], Z, Z[:, :], g_col, None, ALU.mult, rd=[cbuf])
                stt(Z, Z[:, :], ps_u, ps_u[:, 0:129], g_col, Zg, Zg[:, :], ALU.mult, ALU.add, rd=[cbuf])
                cp(Zb, Zb[:, :], Z, Z[:, :])
                yb = yf[j % 2]
                if is_ml:
                    a_col = gm[:, 12 + j:13 + j]
                    ts(st, st[:, 5:6], ps_o, ps_o[:, 128:129], a_col, None, ALU.mult, rd=[gm])
                    ts(st, st[:, 6:7], st, st[:, 5:6], -1.0, None, ALU.mult)
                    tt(st, st[:, 5:6], st, st[:, 5:6], st, st[:, 6:7], ALU.max)
                    ts(st, st[:, 5:6], st, st[:, 5:6], 1.0, None, ALU.max)
                    P.op("dve", lambda e: e.reciprocal(out=st[:, 6:7], in_=st[:, 5:6]), reads=[st], writes=[st])
                    tt(st, st[:, 7:8], st, st[:, 6:7], gm, a_col, ALU.mult)
                    ts(yb, yb[:, :], ps_o, ps_o[:, 0:128], st[:, 7:8], None, ALU.mult, rd=[st])
                else:
                    ts(yb, yb[:, :], ps_o, ps_o[:, 0:128], pr[:, 19:20], None, ALU.mult, rd=[pr])
                finish_block(yb, yb[:, :], ms_col, (gateT, gateT[:, j * 128:(j + 1) * 128]), yt[ti % 2], j)
            store_y(yt[ti % 2], slot, ti)

    P.emit(final_waits=outs)
    return nc


HD = 128
RET_W = 512; FOX_W = 1024; ML_W = 512
O_RQ, O_RK, O_RV, O_RG = 0, 512, 1024, 1536
O_FQ, O_FK, O_FV, O_FF = 2048, 3072, 4096, 5120
O_MQ, O_MK, O_MV, O_MO, O_MI, O_MF = 5128, 5640, 6152, 6664, 7176, 7180


def mixer_consts():
    cst = np.zeros((128, 4, 128), np.float32)
    r = np.arange(128)
    cst[:, 0, :] = (r[:, None] <= r[None, :]).astype(np.float32)
    cst[:, 1, :] = np.where(r[:, None] > r[None, :], -30000.0, 0.0)
    cst[:, 2, :] = np.eye(128, dtype=np.float32)
    cst[:, 3, :] = 1.0
    return cst


def pack_mixer_core(g, w_in, conv_w, conv_b, fox_f_bias, mlstm_i_bias, mlstm_f_bias, merge_scale):
    def hcols(off, h):
        return w_in[:, off + h * HD: off + (h + 1) * HD]
    cols = []
    for fh in (2 * g, 2 * g + 1):
        cols += [hcols(O_FQ, fh), hcols(O_FK, fh), hcols(O_FV, fh), w_in[:, O_FF + fh:O_FF + fh + 1]]
    rq = hcols(O_RQ, g); rk = hcols(O_RK, g)
    perm = np.concatenate([np.arange(64, 128), np.arange(0, 64)])
    cols += [rq, rq[:, perm], rk, rk[:, perm], hcols(O_RV, g), hcols(O_RG, g)]
    cols += [hcols(O_MQ, g), hcols(O_MK, g), hcols(O_MV, g), w_in[:, O_MI + g:O_MI + g + 1], w_in[:, O_MF + g:O_MF + g + 1],
             hcols(O_MO, g)]
    Wc = np.ascontiguousarray(np.concatenate(cols, axis=1), dtype=np.float32)
    assert Wc.shape[1] == 2052, Wc.shape
    prm = np.zeros((128, 24), np.float32)
    prm[:, 0] = fox_f_bias[2 * g]; prm[:, 1] = fox_f_bias[2 * g + 1]
    prm[:, 2] = mlstm_i_bias[g]; prm[:, 3] = mlstm_f_bias[g]
    prm[:, 4:8] = conv_w[:, g * HD:(g + 1) * HD].T
    prm[:, 8:12] = conv_w[:, ML_W + g * HD: ML_W + (g + 1) * HD].T
    prm[:, 12] = conv_b[g * HD:(g + 1) * HD]
    prm[:, 13] = conv_b[ML_W + g * HD: ML_W + (g + 1) * HD]
    prm[:, 14] = merge_scale[g * HD:(g + 1) * HD]
    prm[:, 15] = merge_scale[RET_W + 2 * g * HD: RET_W + (2 * g + 1) * HD]
    prm[:, 16] = merge_scale[RET_W + (2 * g + 1) * HD: RET_W + (2 * g + 2) * HD]
    prm[:, 17] = merge_scale[RET_W + FOX_W + g * HD: RET_W + FOX_W + (g + 1) * HD]
    gamma = 1.0 - 2.0 ** (-5.0 - g)
    p = np.arange(128, dtype=np.float64)
    prm[:, 18] = gamma ** (-p)
    prm[:, 19] = gamma ** p
    prm[:, 20] = gamma ** 128
    inv_freq = (10000.0 ** (-np.arange(0, 128, 2, dtype=np.float32) / 128)).astype(np.float32)
    prm[:, 21] = np.concatenate([inv_freq, inv_freq])
    sgn = np.where(np.arange(128) < 64, -1.0, 1.0)
    prm[:, 22] = sgn
    prm[:, 23] = sgn * (HD ** -0.5)
    return Wc, prm


MOD_F = 3072


def build_L0():
    nc = bass.Bass("TRN2", target_bir_lowering=False)
    c2T = nc.dram_tensor("c2T", [128, 16, 2], F32, kind="ExternalInput").ap()
    Wm = nc.dram_tensor("Wm", [2048, MOD_F], F32, kind="ExternalInput").ap()
    bm = nc.dram_tensor("bm", [128, MOD_F // 128], F32, kind="ExternalInput").ap()
    modT = nc.dram_tensor("modT", [128, MOD_F // 128, 2], F32, kind="ExternalOutput").ap()
    P = Prog(nc)
    cs_ = P.sbuf("cs_", [128, 16, 2], F32)
    cb_ = P.sbuf("cb_", [128, 16, 2], BF16)
    bs_ = P.sbuf("bs_", [128, MOD_F // 128], F32)
    os_ = P.sbuf("os_", [128, MOD_F // 128, 2], F32)
    wr = [P.sbuf(f"wr{i}", [128, 16, 256], BF16) for i in range(4)]
    ps = [P.psum(f"ps{i}", [128, 512], F32) for i in range(4)]
    P.dma("sp", lambda e: [e.dma_start(out=cs_[:, :, :], in_=c2T)], cs_, 1)
    P.dma("sp", lambda e: [e.dma_start(out=bs_[:, :], in_=bm)], bs_, 1)
    P.op("act", lambda e: e.activation(out=cb_[:, :, :], in_=cs_[:, :, :], func=AF.Silu), reads=[cs_], writes=[cb_])
    for blk in range(MOD_F // 256):
        wb = wr[blk % 4]
        P.dma("pool", lambda e, wb=wb, blk=blk: [e.dma_start(out=wb[:, :, :], in_=Wm[:, blk * 256:(blk + 1) * 256].rearrange("(c p) f -> p c f", p=128))], wb, 1)
        for j in range(2):
            ch = blk * 2 + j
            pb = ps[ch % 4]
            for k in range(16):
                P.op("pe", lambda e, wb=wb, pb=pb, k=k, j=j: e.matmul(pb[:, 0:2], wb[:, k, j * 128:(j + 1) * 128], cb_[:, k, :], start=(k == 0), stop=(k == 15)),
                     reads=[wb, cb_], writes=[pb])
            P.op("dve", lambda e, pb=pb, ch=ch: e.tensor_scalar(out=os_[:, ch, :], in0=pb[:, 0:2], scalar1=bs_[:, ch:ch + 1], scalar2=None, op0=ALU.add),
                 reads=[pb, bs_], writes=[os_])
    o = P.dma("sp", lambda e: [e.dma_start(out=modT, in_=os_[:, :, :])], os_, 1)
    P.emit(final_waits=[o])
    return nc


_NC_CACHE = {}


def _get_nc(key, fn):
    if key not in _NC_CACHE:
        _NC_CACHE[key] = fn()
    return _NC_CACHE[key]


def _vec_layout(rows):
    v = np.zeros((10, 2048), np.float32)
    for i, r in enumerate(rows):
        if r is not None:
            v[i] = r
    return np.ascontiguousarray(v.reshape(10, 16, 128).transpose(2, 0, 1))


def kernel(x, c, positions, ada_w, ada_b, norm_mix_w, norm_mlp_w, w_in, conv_w, conv_b,
           fox_f_bias, mlstm_i_bias, mlstm_f_bias, merge_scale, w_out, w_ff1, w_ff2, final_norm_w):
    f32 = lambda a: np.ascontiguousarray(np.asarray(a), dtype=np.float32)
    x = f32(x); c = f32(c); positions = np.asarray(positions).astype(np.int32)
    ada_w = f32(ada_w); ada_b = f32(ada_b); norm_mix_w = f32(norm_mix_w); norm_mlp_w = f32(norm_mlp_w)
    w_in = f32(w_in); conv_w = f32(conv_w); conv_b = f32(conv_b); fox_f_bias = f32(fox_f_bias)
    mlstm_i_bias = f32(mlstm_i_bias); mlstm_f_bias = f32(mlstm_f_bias); merge_scale = f32(merge_scale)
    w_out = f32(w_out); w_ff1 = f32(w_ff1); w_ff2 = f32(w_ff2); final_norm_w = f32(final_norm_w)
    B, S, Dm = x.shape
    NCORE = 8
    cores = list(range(NCORE))
    TOK = S // 4

    nc0 = _get_nc("L0", build_L0)
    c2T = np.ascontiguousarray(c.T.reshape(16, 128, B).transpose(1, 0, 2))
    in_maps = []
    for i in cores:
        l, fq = i // 4, i % 4
        f0 = fq * MOD_F
        in_maps.append({"c2T": c2T,
                        "Wm": np.ascontiguousarray(ada_w[l][:, f0:f0 + MOD_F]),
                        "bm": np.ascontiguousarray(ada_b[l][f0:f0 + MOD_F].reshape(MOD_F // 128, 128).T)})
    res = run_bass_kernel_spmd(nc0, in_maps, core_ids=cores).results
    mod = np.zeros((2, B, 6 * 2048), np.float32)
    for i in cores:
        l, fq = i // 4, i % 4
        m = np.asarray(res[i]["modT"])
        mod[l, :, fq * MOD_F:(fq + 1) * MOD_F] = m.transpose(2, 1, 0).reshape(B, MOD_F)
    sp = lambda l, b: np.split(mod[l, b], 6)

    xT = [np.ascontiguousarray(x[b].T) for b in range(B)]

    ncH = _get_nc("H", lambda: build_F(TOK, False, False))
    in_maps = []
    for i in cores:
        b, q = i // 4, i % 4
        sh_a, sc_a = sp(0, b)[0], sp(0, b)[1]
        in_maps.append({"xT": np.ascontiguousarray(xT[b][:, q * TOK:(q + 1) * TOK]),
                        "vec": _vec_layout([None] * 5 + [norm_mix_w[0], sc_a, sh_a])})
    res = run_bass_kernel_spmd(ncH, in_maps, core_ids=cores).results
    hT = [np.concatenate([np.asarray(res[b * 4 + q]["h_out"]) for q in range(4)], axis=1) for b in range(B)]

    cstm = mixer_consts()
    posr = [np.ascontiguousarray(np.broadcast_to(positions[b][None, :], (128, S))).astype(np.int32) for b in range(B)]
    out = None
    for l in range(2):
        ncM = _get_nc("M", lambda: build_M(S))
        in_maps = []
        for i in cores:
            b, g = i // 4, i % 4
            Wc, prm = pack_mixer_core(g, w_in[l], conv_w[l], conv_b[l], fox_f_bias[l], mlstm_i_bias[l],
                                      mlstm_f_bias[l], merge_scale[l])
            in_maps.append({"hT": np.ascontiguousarray(hT[b]), "posr": posr[b], "Wc": Wc, "prm": prm, "cst": cstm})
        res = run_bass_kernel_spmd(ncM, in_maps, core_ids=cores).results
        yT = [np.zeros((2048, S), ml_dtypes.bfloat16) for _ in range(B)]
        for i in cores:
            b, g = i // 4, i % 4
            y = np.asarray(res[i]["yT"])
            yT[b][g * 128:(g + 1) * 128] = y[0:128]
            yT[b][512 + 2 * g * 128:512 + (2 * g + 2) * 128] = y[128:384]
            yT[b][1536 + g * 128:1536 + (g + 1) * 128] = y[384:512]
        final = (l == 1)
        ncF = _get_nc(("F", final), lambda: build_F(TOK, True, final))
        in_maps = []
        for i in cores:
            b, q = i // 4, i % 4
            sh_a, sc_a, g_a, sh_m, sc_m, g_m = sp(l, b)
            if final:
                rows = [g_a, norm_mlp_w[l], sc_m, sh_m, g_m, final_norm_w]
            else:
                nsh_a, nsc_a = sp(l + 1, b)[0], sp(l + 1, b)[1]
                rows = [g_a, norm_mlp_w[l], sc_m, sh_m, g_m, norm_mix_w[l + 1], nsc_a, nsh_a]
            in_maps.append({"xT": np.ascontiguousarray(xT[b][:, q * TOK:(q + 1) * TOK]),
                            "yT": np.ascontiguousarray(yT[b][:, q * TOK:(q + 1) * TOK]),
                            "vec": _vec_layout(rows),
                            "w_out": w_out[l], "w_ff1": w_ff1[l], "w_ff2": w_ff2[l]})
        res = run_bass_kernel_spmd(ncF, in_maps, core_ids=cores).results
        if final:
            out = np.zeros((B, S, Dm), np.float32)
            for i in cores:
                b, q = i // 4, i % 4
                out[b, q * TOK:(q + 1) * TOK, :] = np.asarray(res[i]["h_out"]).T
        else:
            xT = [np.concatenate([np.asarray(res[b * 4 + q]["x_out"]) for q in range(4)], axis=1) for b in range(B)]
            hT = [np.concatenate([np.asarray(res[b * 4 + q]["h_out"]) for q in range(4)], axis=1) for b in range(B)]
    return out
```
